# Optimizing a Trainium2 kernel written in Bass

```python
import jax, jax.numpy as jnp
from jax import lax
import numpy as np

D_MODEL = 1024
BATCH = 2
SEQ = 8192
DEPTH = 2

HEAD_DIM = 64
MLA_HEADS = 4
MLA_Q_LORA = 256
MLA_KV_LORA = 128
MLA_NOPE = 64
MLA_ROPE = 32
MLA_V = 64
DIL_HEADS = 6
DIL_PATTERNS = ((128, 1), (512, 4), (2048, 16))
NSA_HEADS = 6
NSA_KV_GROUPS = 2
NSA_CMP_BLOCK = 32
NSA_CMP_STRIDE = 16
NSA_CMP_HIDDEN = 256
NSA_SEL_BLOCK = 64
NSA_TOP_N = 16
NSA_WINDOW = 512

Q_BLOCK = 128
MIX_WIDTH = (MLA_HEADS + DIL_HEADS + NSA_HEADS) * HEAD_DIM
D_FF = ((8 * D_MODEL // 3 + 255) // 256) * 256
ROPE_THETA = 10000.0
LN_EPS = 1e-5
RMS_EPS = 1e-6
NEG_INF = -1e30
FORCE_SCORE = 1e4

IN_SPLITS = (
    MLA_Q_LORA, MLA_KV_LORA, MLA_ROPE,
    DIL_HEADS * HEAD_DIM, DIL_HEADS * HEAD_DIM, DIL_HEADS * HEAD_DIM,
    NSA_HEADS * HEAD_DIM,
) + (NSA_KV_GROUPS * HEAD_DIM,) * 6 + (NSA_HEADS * 3,)
IN_WIDTH = sum(IN_SPLITS)

kernel_name = 'hybrid_mla_dilated_nsa_macaron_deepnorm'


def layer_norm(x, g, b):
    xf = x.astype(jnp.float32)
    mu = jnp.mean(xf, axis=-1, keepdims=True)
    var = jnp.mean(jnp.square(xf - mu), axis=-1, keepdims=True)
    return ((xf - mu) * lax.rsqrt(var + LN_EPS) * g + b).astype(x.dtype)


def rms_norm(x, g):
    xf = x.astype(jnp.float32)
    return (xf * lax.rsqrt(jnp.mean(xf * xf, axis=-1, keepdims=True) + RMS_EPS) * g).astype(x.dtype)


def swiglu(x, w_in, w_out):
    gate, up = jnp.split(x @ w_in, 2, axis=-1)
    return (jax.nn.silu(gate) * up) @ w_out


def rope_cos_sin(n_pos, dim):
    inv_freq = ROPE_THETA ** (-jnp.arange(0, dim, 2, dtype=jnp.float32) / dim)
    ang = jnp.arange(n_pos, dtype=jnp.float32)[:, None] * inv_freq[None, :]
    return jnp.cos(ang), jnp.sin(ang)


def apply_rope(x, cos, sin):
    half = x.shape[-1] // 2
    x1 = x[..., :half].astype(jnp.float32)
    x2 = x[..., half:].astype(jnp.float32)
    c, s = cos[:, None, :], sin[:, None, :]
    return jnp.concatenate([x1 * c - x2 * s, x1 * s + x2 * c], axis=-1).astype(x.dtype)


def masked_softmax(s, mask):
    s = jnp.where(mask, s.astype(jnp.float32), NEG_INF)
    m = jnp.max(s, axis=-1, keepdims=True)
    e = jnp.where(mask, jnp.exp(s - m), 0.0)
    l = jnp.maximum(jnp.sum(e, axis=-1, keepdims=True), 1e-30)
    return e / l, (m + jnp.log(l))[..., 0]


def split_cols(h, sizes):
    out, start = [], 0
    for n in sizes:
        out.append(h[..., start:start + n])
        start += n
    return out


def causal_block_attention(q, k, v):
    B, S, H, dq = q.shape
    nb = S // Q_BLOCK
    qb = q.reshape(B, nb, Q_BLOCK, H, dq).transpose(1, 0, 2, 3, 4)
    key_pos = jnp.arange(S)
    scale = dq ** -0.5

    def one_block(args):
        i, q_blk = args
        s = jnp.einsum('bqhd,bkhd->bhqk', q_blk, k).astype(jnp.float32) * scale
        qpos = i * Q_BLOCK + jnp.arange(Q_BLOCK)
        p, _ = masked_softmax(s, key_pos[None, :] <= qpos[:, None])
        return jnp.einsum('bhqk,bkhd->bqhd', p.astype(v.dtype), v)

    o = lax.map(one_block, (jnp.arange(nb), qb))
    return o.transpose(1, 0, 2, 3, 4).reshape(B, S, H, v.shape[-1])


def banded_attention(q, k, v, window):
    L, d = q.shape[-2], q.shape[-1]
    nb = L // Q_BLOCK
    n_prev = -(-window // Q_BLOCK)

    def blocks(t):
        return t.reshape(t.shape[:-2] + (nb, Q_BLOCK, t.shape[-1]))

    def with_history(tb):
        pad = [(0, 0)] * (tb.ndim - 3) + [(n_prev, 0), (0, 0), (0, 0)]
        tp = jnp.pad(tb, pad)
        return jnp.concatenate([tp[..., j:j + nb, :, :] for j in range(n_prev + 1)], axis=-2)

    qb = blocks(q)
    kh, vh = with_history(blocks(k)), with_history(blocks(v))
    s = jnp.matmul(qb, jnp.swapaxes(kh, -1, -2)).astype(jnp.float32) * d ** -0.5
    qi = jnp.arange(Q_BLOCK)[:, None] + n_prev * Q_BLOCK
    kj = jnp.arange((n_prev + 1) * Q_BLOCK)[None, :]
    dist = qi - kj
    key_pos = jnp.arange(nb)[:, None, None] * Q_BLOCK - n_prev * Q_BLOCK + kj[None]
    mask = (dist >= 0) & (dist <= window) & (key_pos >= 0)
    p, lse = masked_softmax(s, mask)
    o = jnp.matmul(p.astype(vh.dtype), vh)
    return o.reshape(o.shape[:-3] + (L, d)), lse.reshape(lse.shape[:-2] + (L,))


def dilated_attention(q, k, v):
    B, S, H, d = q.shape
    outs, lses = [], []
    for window, dil in DIL_PATTERNS:
        span = dil * Q_BLOCK
        Sp = -(-S // span) * span

        def to_sub(t):
            t = jnp.pad(t, ((0, 0), (0, Sp - S), (0, 0), (0, 0)))
            return t.reshape(B, Sp // dil, dil, H, d).transpose(0, 2, 3, 1, 4)

        o, lse = banded_attention(to_sub(q), to_sub(k), to_sub(v), window // dil)
        outs.append(o.transpose(0, 3, 1, 2, 4).reshape(B, Sp, H, d)[:, :S])
        lses.append(lse.transpose(0, 3, 1, 2).reshape(B, Sp, H)[:, :S])
    w = jax.nn.softmax(jnp.stack(lses, axis=-1), axis=-1)
    return jnp.einsum('bshp,pbshd->bshd', w.astype(q.dtype), jnp.stack(outs))


def nsa_attention(q, k_cmp, v_cmp, k_slc, v_slc, k_win, v_win, gates,
                  cmp_pos, cmp_w1, cmp_w2, cos, sin):
    B, S, H, d = q.shape
    G, Hg = NSA_KV_GROUPS, NSA_HEADS // NSA_KV_GROUPS
    nb = S // Q_BLOCK
    scale = d ** -0.5
    q_rot = apply_rope(q, cos, sin)

    n_cmp = (S - NSA_CMP_BLOCK) // NSA_CMP_STRIDE + 1
    cmp_start = jnp.arange(n_cmp) * NSA_CMP_STRIDE
    cmp_end = cmp_start + NSA_CMP_BLOCK - 1
    cmp_idx = cmp_start[:, None] + jnp.arange(NSA_CMP_BLOCK)[None, :]

    def compress(t, pos, w1, w2):
        blk = t[:, cmp_idx] + pos[:, None, :]
        flat = blk.transpose(0, 1, 3, 2, 4).reshape(B, n_cmp, G, NSA_CMP_BLOCK * d)
        return (jax.nn.silu(flat @ w1) @ w2).transpose(0, 2, 1, 3)

    kc = compress(k_cmp, cmp_pos[0], cmp_w1[0], cmp_w2[0])
    vc = compress(v_cmp, cmp_pos[1], cmp_w1[1], cmp_w2[1])

    n_sel = S // NSA_SEL_BLOCK
    n_top = min(NSA_TOP_N, n_sel)
    sel_start = jnp.arange(n_sel) * NSA_SEL_BLOCK
    overlap = ((cmp_start[:, None] < sel_start[None, :] + NSA_SEL_BLOCK) &
               (cmp_end[:, None] >= sel_start[None, :])).astype(jnp.float32)
    ks = k_slc.transpose(0, 2, 1, 3)
    vs = v_slc.transpose(0, 2, 1, 3)
    sel_offsets = jnp.arange(NSA_SEL_BLOCK)
    blk_ids = jnp.arange(n_sel)[None, :]

    def grouped(t):
        return t.reshape(B, nb, Q_BLOCK, G, Hg, d).transpose(1, 0, 3, 4, 2, 5)

    def one_block(args):
        i, q_c, q_s = args
        qpos = i * Q_BLOCK + jnp.arange(Q_BLOCK)
        s = jnp.einsum('bghqd,bgcd->bghqc', q_c, kc).astype(jnp.float32) * scale
        p, _ = masked_softmax(s, cmp_end[None, :] <= qpos[:, None])
        o_c = jnp.einsum('bghqc,bgcd->bghqd', p.astype(vc.dtype), vc)
        imp = jnp.einsum('bgqc,cj->bgqj', jnp.sum(p, axis=2), overlap)
        cur = (qpos // NSA_SEL_BLOCK)[:, None]
        forced = (blk_ids == 0) | (blk_ids == cur) | (blk_ids == cur - 1)
        score = jnp.where(forced, FORCE_SCORE, jnp.where(blk_ids <= cur, imp, -FORCE_SCORE))
        _, top = lax.top_k(score, n_top)
        tok = (top[..., None] * NSA_SEL_BLOCK + sel_offsets).reshape(B, G, -1)
        n_keys = n_top * NSA_SEL_BLOCK
        kg = jnp.take_along_axis(ks, tok[..., None], axis=2).reshape(B, G, Q_BLOCK, n_keys, d)
        vg = jnp.take_along_axis(vs, tok[..., None], axis=2).reshape(B, G, Q_BLOCK, n_keys, d)
        s = jnp.einsum('bghqd,bgqkd->bghqk', q_s, kg).astype(jnp.float32) * scale
        kmask = tok.reshape(B, G, 1, Q_BLOCK, n_keys) <= qpos[:, None]
        p, _ = masked_softmax(s, kmask)
        o_s = jnp.einsum('bghqk,bgqkd->bghqd', p.astype(vg.dtype), vg)
        return o_c, o_s

    o_cmp, o_slc = lax.map(one_block, (jnp.arange(nb), grouped(q), grouped(q_rot)))

    def ungroup(t):
        return t.transpose(1, 0, 4, 2, 3, 5).reshape(B, S, H, d)

    q_w = q_rot.reshape(B, S, G, Hg, d).transpose(0, 2, 3, 1, 4)
    k_w = k_win.transpose(0, 2, 1, 3)[:, :, None]
    v_w = v_win.transpose(0, 2, 1, 3)[:, :, None]
    o_win, _ = banded_attention(q_w, k_w, v_w, NSA_WINDOW)
    o_win = o_win.transpose(0, 3, 1, 2, 4).reshape(B, S, H, d)
    g = gates.astype(q.dtype)
    return g[..., 0:1] * ungroup(o_cmp) + g[..., 1:2] * ungroup(o_slc) + g[..., 2:3] * o_win


def token_mixing(h, w_in, w_out, q_norm, kv_norm, w_uq, w_ukv, cmp_pos, cmp_w1, cmp_w2,
                 cos, sin, cos_m, sin_m):
    B, S, _ = h.shape
    (c_q, c_kv, k_pe, dq, dk, dv, nq, kc, vc, ksl, vsl, kw, vw, gl) = split_cols(h @ w_in, IN_SPLITS)

    def heads(t):
        return t.reshape(B, S, -1, HEAD_DIM)

    q_a = (rms_norm(c_q, q_norm) @ w_uq).reshape(B, S, MLA_HEADS, MLA_NOPE + MLA_ROPE)
    q_a = jnp.concatenate([q_a[..., :MLA_NOPE], apply_rope(q_a[..., MLA_NOPE:], cos_m, sin_m)], axis=-1)
    kv_a = (rms_norm(c_kv, kv_norm) @ w_ukv).reshape(B, S, MLA_HEADS, MLA_NOPE + MLA_V)
    k_pe = apply_rope(k_pe.reshape(B, S, 1, MLA_ROPE), cos_m, sin_m)
    k_a = jnp.concatenate([kv_a[..., :MLA_NOPE],
                           jnp.broadcast_to(k_pe, (B, S, MLA_HEADS, MLA_ROPE))], axis=-1)
    o_a = causal_block_attention(q_a, k_a, kv_a[..., MLA_NOPE:])

    o_b = dilated_attention(apply_rope(heads(dq), cos, sin), apply_rope(heads(dk), cos, sin), heads(dv))

    o_c = nsa_attention(heads(nq), heads(kc), heads(vc),
                        apply_rope(heads(ksl), cos, sin), heads(vsl),
                        apply_rope(heads(kw), cos, sin), heads(vw),
                        jax.nn.sigmoid(gl.astype(jnp.float32)).reshape(B, S, NSA_HEADS, 3),
                        cmp_pos, cmp_w1, cmp_w2, cos, sin)

    o = jnp.concatenate([o_a, o_b, o_c], axis=2).reshape(B, S, MIX_WIDTH)
    return o @ w_out


def setup_inputs(seed: int = 0) -> dict:
    key = jax.random.key(seed)
    k = jax.random.split(key, 14)
    beta = (8 * DEPTH) ** -0.25

    def nrm(kk, shape, scale):
        return jax.random.normal(kk, shape, jnp.float32) * scale

    return {
        'x': nrm(k[0], (BATCH, SEQ, D_MODEL), 1.0),
        'ffn_w_in': nrm(k[1], (DEPTH, 2, D_MODEL, 2 * D_FF), D_MODEL ** -0.5),
        'ffn_w_out': nrm(k[2], (DEPTH, 2, D_FF, D_MODEL), beta * D_FF ** -0.5),
        'ln_gain': 1.0 + nrm(k[3], (DEPTH, 3, D_MODEL), 0.02),
        'ln_bias': nrm(k[4], (DEPTH, 3, D_MODEL), 0.02),
        'w_in': nrm(k[5], (DEPTH, D_MODEL, IN_WIDTH), D_MODEL ** -0.5),
        'w_out': nrm(k[6], (DEPTH, MIX_WIDTH, D_MODEL), beta * MIX_WIDTH ** -0.5),
        'mla_q_norm': 1.0 + nrm(k[7], (DEPTH, MLA_Q_LORA), 0.02),
        'mla_kv_norm': 1.0 + nrm(k[8], (DEPTH, MLA_KV_LORA), 0.02),
        'mla_w_uq': nrm(k[9], (DEPTH, MLA_Q_LORA, MLA_HEADS * (MLA_NOPE + MLA_ROPE)), MLA_Q_LORA ** -0.5),
        'mla_w_ukv': nrm(k[10], (DEPTH, MLA_KV_LORA, MLA_HEADS * (MLA_NOPE + MLA_V)), MLA_KV_LORA ** -0.5),
        'nsa_cmp_pos': nrm(k[11], (DEPTH, 2, NSA_CMP_BLOCK, HEAD_DIM), 0.1),
        'nsa_cmp_w1': nrm(k[12], (DEPTH, 2, NSA_CMP_BLOCK * HEAD_DIM, NSA_CMP_HIDDEN),
                          (NSA_CMP_BLOCK * HEAD_DIM) ** -0.5),
        'nsa_cmp_w2': nrm(k[13], (DEPTH, 2, NSA_CMP_HIDDEN, HEAD_DIM), NSA_CMP_HIDDEN ** -0.5),
    }


def reference(x, ffn_w_in, ffn_w_out, ln_gain, ln_bias, w_in, w_out, mla_q_norm, mla_kv_norm,
              mla_w_uq, mla_w_ukv, nsa_cmp_pos, nsa_cmp_w1, nsa_cmp_w2):
    S = x.shape[1]
    alpha = (2 * DEPTH) ** 0.25
    cos, sin = rope_cos_sin(S, HEAD_DIM)
    cos_m, sin_m = rope_cos_sin(S, MLA_ROPE)
    for l in range(DEPTH):
        x = layer_norm(alpha * x + 0.5 * swiglu(x, ffn_w_in[l, 0], ffn_w_out[l, 0]),
                       ln_gain[l, 0], ln_bias[l, 0])
        mix = token_mixing(x, w_in[l], w_out[l], mla_q_norm[l], mla_kv_norm[l], mla_w_uq[l],
                           mla_w_ukv[l], nsa_cmp_pos[l], nsa_cmp_w1[l], nsa_cmp_w2[l],
                           cos, sin, cos_m, sin_m)
        x = layer_norm(alpha * x + mix, ln_gain[l, 1], ln_bias[l, 1])
        x = layer_norm(alpha * x + 0.5 * swiglu(x, ffn_w_in[l, 1], ffn_w_out[l, 1]),
                       ln_gain[l, 2], ln_bias[l, 2])
    return x
```

```python
import contextlib
import math

import ml_dtypes
import numpy as np

import concourse.bass as bass
import concourse.mybir as mybir
from concourse.bass_utils import run_bass_kernel_spmd

F32 = mybir.dt.float32
BF16 = mybir.dt.bfloat16
AF = mybir.ActivationFunctionType
ALU = mybir.AluOpType
AX = mybir.AxisListType

D_MODEL = 1024
SEQ = 8192
DEPTH = 2
D_FF = 2816
NFC = D_FF // 128
KC = D_MODEL // 128
NT = 16
TOK = NT * 128
NR = 4
LN_EPS = 1e-5
RMS_EPS = 1e-6
ALPHA = (2 * DEPTH) ** 0.25
NEG = -30000.0
IN_W = 2738

O_CQ, O_CKV, O_KPE, O_DQ, O_DK, O_DV, O_NQ, O_KC, O_VC, O_KSL, O_VSL, O_KW, O_VW, O_GL = (
    0, 256, 384, 416, 800, 1184, 1568, 1952, 2080, 2208, 2336, 2464, 2592, 2720)

XK_MLA, XK_DIL, XK_KC, XK_VC, XK_KS, XK_KW, XK_ROWS = 0, 384, 768, 896, 1024, 1152, 1280
XV_MLA, XV_DIL, XV_NSA, XV_COLS = 0, 260, 650, 910
Q_MLA, Q_DIL, Q_NROT, Q_NRAW, Q_ROWS = 0, 384, 768, 1152, 1536


class Buf:
    __slots__ = ("w", "r", "name", "excl")

    def __init__(self, name="", excl=False):
        self.w = None
        self.r = {}
        self.name = name
        self.excl = excl


def _split(r, w):
    if any(b.excl for b in r):
        w = list(w) + [b for b in r if b.excl and b not in w]
        r = [b for b in r if not b.excl]
    return r, w


ENGS = ("tensor", "vector", "scalar", "gpsimd", "sync")
NDS = 6


class Prog:
    def __init__(self, nc, es):
        self.nc = nc
        self.ops = {e: [] for e in ENGS}
        self.sem = {e: es.enter_context(nc.semaphore("c_" + e)) for e in ENGS}
        self.cnt = {e: 0 for e in ENGS}
        self.waited = {}
        self.dsem = {}
        self.dval = {}
        self.dnext = {}
        for q in ("sync", "gpsimd", "scalar"):
            self.dsem[q] = [es.enter_context(nc.semaphore("d_%s%d" % (q, i))) for i in range(NDS)]
            self.dval[q] = [0] * NDS
            self.dnext[q] = 0
        self.semobj = {}
        for e in ENGS:
            self.semobj[id(self.sem[e])] = self.sem[e]
        for q in self.dsem:
            for s in self.dsem[q]:
                self.semobj[id(s)] = s

    def _wait(self, eng, ev):
        s, v = ev
        key = (eng, id(s))
        if self.waited.get(key, 0) >= v:
            return
        self.waited[key] = v
        self.ops[eng].append(("w", s, v))

    def _collect(self, eng, r, w):
        evs = []
        for b in r:
            if b.w is not None:
                evs.append(b.w)
        for b in w:
            if b.w is not None:
                evs.append(b.w)
            evs.extend(b.r.values())
        own = id(self.sem[eng])
        for ev in evs:
            if eng == "tensor" and id(ev[0]) == own:
                continue
            self._wait(eng, ev)

    def _mark(self, ev, r, w):
        k = id(ev[0])
        for b in r:
            o = b.r.get(k)
            if o is None or o[1] < ev[1]:
                b.r[k] = ev
        for b in w:
            b.w = ev
            b.r = {}

    def op(self, eng, fn, r=(), w=()):
        r, w = _split(r, w)
        self._collect(eng, r, w)
        self.cnt[eng] += 1
        ev = (self.sem[eng], self.cnt[eng])
        self.ops[eng].append(("o", fn, self.sem[eng], 1))
        self._mark(ev, r, w)
        return ev

    def dma(self, q, out, in_, r=(), w=()):
        r, w = _split(r, w)
        self._collect(q, r, w)
        i = self.dnext[q]
        self.dnext[q] = (i + 1) % NDS
        s = self.dsem[q][i]
        if self.dval[q][i] > 0:
            self._wait(q, (s, self.dval[q][i]))
        self.dval[q][i] += 16
        ev = (s, self.dval[q][i])
        self.ops[q].append(("o", lambda e, o=out, a=in_: e.dma_start(out=o, in_=a), s, 16))
        self._mark(ev, r, w)
        return ev

    def barrier(self):
        evs = [(self.sem[e], self.cnt[e]) for e in ENGS if self.cnt[e] > 0]
        for q in self.dsem:
            for i, s in enumerate(self.dsem[q]):
                if self.dval[q][i] > 0:
                    evs.append((s, self.dval[q][i]))
        for e in ENGS:
            for ev in evs:
                self._wait(e, ev)

    def emit(self, block):
        def run(eng):
            def f(e):
                for it in self.ops[eng]:
                    if it[0] == "w":
                        e.wait_ge(it[1], it[2])
                    else:
                        it[1](e).then_inc(it[2], it[3])
            return f
        block.tensor(run("tensor"))
        block.vector(run("vector"))
        block.scalar(run("scalar"))
        block.gpsimd(run("gpsimd"))
        block.sync(run("sync"))

    def mm(self, out, lhsT, rhs, start, stop, r, w):
        return self.op("tensor", lambda e: e.matmul(out, lhsT=lhsT, rhs=rhs, start=start, stop=stop,
                                                    skip_group_check=True), r, w)

    def tr(self, out, in_, ident, r, w):
        return self.op("tensor", lambda e: e.transpose(out, in_, ident), r, w)

    def act(self, out, in_, func, r, w, bias=None, scale=1.0, accum_out=None):
        kw = {}
        if bias is not None:
            kw["bias"] = bias
        if accum_out is not None:
            kw["accum_out"] = accum_out
        return self.op("scalar", lambda e: e.activation(out=out, in_=in_, func=func, scale=scale, **kw), r, w)

    def v(self, fn, r, w):
        return self.op("vector", fn, r, w)

    def g(self, fn, r, w):
        return self.op("gpsimd", fn, r, w)


class Rot:
    def __init__(self, items):
        self.items = items
        self.i = 0

    def next(self):
        it = self.items[self.i]
        self.i = (self.i + 1) % len(self.items)
        return it


class Arena:
    def __init__(self, tensor, ncols):
        self.t = tensor
        self.n = ncols
        self.off = 0

    def f32(self, cols):
        a = self.t[:, self.off:self.off + cols]
        self.off += cols
        assert self.off <= self.n, "SBUF arena overflow %d > %d" % (self.off, self.n)
        return a

    def bf(self, cols):
        c = (cols + 1) // 2
        return self.f32(c).bitcast(BF16)[:, 0:cols]

    def mark(self):
        return self.off

    def reset(self, m):
        self.off = m


class Builder:
    def __init__(self, phases, fused):
        self.phases = phases
        self.fused = fused
        self.nc = bass.Bass("TRN2", target_bir_lowering=False)
        self.din = {}
        self.dout = {}

    def inp(self, name, shape, dt=F32):
        t = self.nc.dram_tensor(name, list(shape), dt, kind="ExternalInput").ap()
        self.din[name] = t
        return t

    def outp(self, name, shape, dt=F32):
        t = self.nc.dram_tensor(name, list(shape), dt, kind="ExternalOutput").ap()
        self.dout[name] = t
        return t

    def scratch(self, name, shape, dt=F32):
        return self.nc.dram_tensor(name, list(shape), dt, kind="Internal").ap()


class Ctx:
    pass


def setup_common(B, es):
    nc = B.nc
    C = Ctx()
    C.B = B
    C.nc = nc
    C.P = Prog(nc, es)
    NCOLS = 52736
    arena = es.enter_context(nc.sbuf_tensor("arena", [128, NCOLS], F32))
    C.A = Arena(arena, NCOLS)
    C.ps = []
    C.psb = []
    for i in range(8):
        t = es.enter_context(nc.psum_tensor("ps%d" % i, [128, 512], F32))
        C.ps.append(t)
        C.psb.append(Buf("ps%d" % i, excl=True))
    A = C.A
    C.x_tok = A.f32(NT * D_MODEL).rearrange("p (j d) -> p j d", d=D_MODEL)
    C.xb = [Buf("x%d" % j) for j in range(NT)]
    C.xT_raw = A.bf(KC * TOK)
    C.xT = C.xT_raw.rearrange("p (k t) -> p k t", t=TOK)
    C.xTb = [Buf("xT%d" % j) for j in range(4)]
    C.ident_bf = A.bf(128)
    C.ident_f = A.f32(128)
    C.ones_f = A.f32(128)
    C.ones_bf = A.bf(128)
    C.eps_ln = A.f32(1)
    C.eps_q = A.f32(1)
    C.eps_kv = A.f32(1)
    C.cb = Buf("consts")
    P = C.P
    d_ident_bf = B.inp("c_ident_bf", [128, 128], BF16)
    d_ident_f = B.inp("c_ident_f", [128, 128], F32)
    P.dma("sync", C.ident_bf, d_ident_bf[:, :], w=[C.cb])
    P.dma("sync", C.ident_f, d_ident_f[:, :], w=[C.cb])
    P.v(lambda e: e.memset(C.ones_f, 1.0), [], [C.cb])
    P.v(lambda e: e.memset(C.ones_bf, 1.0), [], [C.cb])
    P.v(lambda e: e.memset(C.eps_ln, LN_EPS), [], [C.cb])
    P.v(lambda e: e.memset(C.eps_q, 256.0 * RMS_EPS), [], [C.cb])
    P.v(lambda e: e.memset(C.eps_kv, 128.0 * RMS_EPS), [], [C.cb])
    return C


def psbf(C, i):
    return C.ps[i][:].bitcast(BF16)


def make_xT(C, j, xbf_rot):
    P = C.P
    xbf, xbfb = xbf_rot.next()
    P.act(xbf, C.x_tok[:, j, :], AF.Copy, [C.xb[j]], [xbfb])
    pv = psbf(C, 6)
    for k in range(KC):
        P.tr(pv[:, k * 128:(k + 1) * 128], xbf[:, k * 128:(k + 1) * 128], C.ident_bf, [xbfb, C.cb], [C.psb[6]])
    dst = C.xT[:, :, j * 128:(j + 1) * 128]
    src = pv.rearrange("p (k t) -> p k t", t=128)
    P.v(lambda e: e.tensor_copy(out=dst, in_=src), [C.psb[6]], [C.xTb[j // 4]])


def phase_load_x(C, x_d):
    P = C.P
    A = C.A
    m = A.mark()
    xbf_rot = Rot([(A.bf(D_MODEL), Buf()) for _ in range(2)])
    for j in range(NT):
        P.dma("sync", C.x_tok[:, j, :], x_d[j * 128:(j + 1) * 128, :], w=[C.xb[j]])
    for j in range(NT):
        make_xT(C, j, xbf_rot)
    P.barrier()
    A.reset(m)


def phase_ffn(C, w_in_d, w_out_d, gain_d, bias_d, want_xT=True):
    P = C.P
    A = C.A
    ps, psb = C.ps, C.psb
    m = A.mark()
    HF = NFC // 2
    hT = A.bf(HF * TOK).rearrange("p (f t) -> p f t", t=TOK)
    hTb = [Buf("hT%d" % i) for i in range(4)]
    wo = A.bf(HF * D_MODEL).rearrange("p (f d) -> p f d", d=D_MODEL)
    wob = [Buf("wo%d" % i) for i in range(HF)]
    wi_rot = Rot([(A.bf(KC * 256).rearrange("p (k c) -> p k c", c=256), Buf()) for _ in range(3)])
    sg_rot = Rot([(A.f32(512), Buf()) for _ in range(2)])
    xbf_rot = Rot([(A.bf(D_MODEL), Buf()) for _ in range(2)])
    gain = A.f32(D_MODEL)
    bias = A.f32(D_MODEL)
    gb = Buf("gainbias")
    st_rot = Rot([(A.f32(16), Buf()) for _ in range(2)])
    P.dma("sync", gain, gain_d.partition_broadcast(128), w=[gb])
    P.dma("sync", bias, bias_d.partition_broadcast(128), w=[gb])
    gu_rot = Rot([(0, 1), (2, 3)])
    o_rot = Rot([4, 5])
    for hh in range(2):
        for fl in range(HF):
            fc = hh * HF + fl
            P.dma("gpsimd", wo[:, fl, :], w_out_d[fc * 128:(fc + 1) * 128, :], w=[wob[fl]])
        for fl in range(HF):
            fc = hh * HF + fl
            wi, wib = wi_rot.next()
            P.dma("gpsimd", wi[:, :, 0:128],
                  w_in_d[:, fc * 128:(fc + 1) * 128].rearrange("(k p) c -> p k c", p=128), w=[wib])
            P.dma("gpsimd", wi[:, :, 128:256],
                  w_in_d[:, D_FF + fc * 128:D_FF + (fc + 1) * 128].rearrange("(k p) c -> p k c", p=128), w=[wib])
            for tg in range(4):
                bg, bu = gu_rot.next()
                ts = slice(tg * 512, (tg + 1) * 512)
                for k in range(KC):
                    P.mm(ps[bg][:], wi[:, k, 0:128], C.xT[:, k, ts], k == 0, k == KC - 1,
                         [wib, C.xTb[tg]], [psb[bg]])
                for k in range(KC):
                    P.mm(ps[bu][:], wi[:, k, 128:256], C.xT[:, k, ts], k == 0, k == KC - 1,
                         [wib, C.xTb[tg]], [psb[bu]])
                sg, sgb = sg_rot.next()
                P.act(sg, ps[bg][:], AF.Silu, [psb[bg]], [sgb])
                dst = hT[:, fl, ts]
                pu = ps[bu][:]
                P.v(lambda e, d=dst, s=sg, u=pu: e.scalar_tensor_tensor(
                    out=d, in0=s, scalar=0.5, in1=u, op0=ALU.mult, op1=ALU.mult), [sgb, psb[bu]], [hTb[tg]])
        for tt in range(NT):
            for nh in range(2):
                bo = o_rot.next()
                cs = slice(nh * 512, (nh + 1) * 512)
                for fl in range(HF):
                    P.mm(ps[bo][:], hT[:, fl, tt * 128:(tt + 1) * 128], wo[:, fl, cs], fl == 0, fl == HF - 1,
                         [hTb[tt // 4], wob[fl]], [psb[bo]])
                xs = C.x_tok[:, tt, cs]
                po = ps[bo][:]
                if hh == 0:
                    P.v(lambda e, x=xs, p=po: e.scalar_tensor_tensor(
                        out=x, in0=x, scalar=ALPHA, in1=p, op0=ALU.mult, op1=ALU.add), [psb[bo]], [C.xb[tt]])
                else:
                    P.v(lambda e, x=xs, p=po: e.tensor_tensor(out=x, in0=x, in1=p, op=ALU.add),
                        [psb[bo]], [C.xb[tt]])
            if hh == 1:
                layer_norm_tile(C, tt, gain, bias, gb, st_rot)
                if want_xT:
                    make_xT(C, tt, xbf_rot)
    P.barrier()
    A.reset(m)


def layer_norm_tile(C, j, gain, bias, gb, st_rot):
    P = C.P
    y = C.x_tok[:, j, :]
    st, stb = st_rot.next()
    s6 = st[:, 0:12].rearrange("p (a b) -> p a b", b=6)
    mv = st[:, 12:14]
    rs = st[:, 14:15]
    nb = st[:, 15:16]
    for c in range(2):
        P.v(lambda e, c=c: e.bn_stats(out=s6[:, c, :], in_=y[:, c * 512:(c + 1) * 512]), [C.xb[j]], [stb])
    P.v(lambda e: e.bn_aggr(out=mv, in_=s6), [stb], [stb])
    P.act(rs, mv[:, 1:2], AF.Ln, [stb, C.cb], [stb], bias=C.eps_ln)
    P.act(rs, rs, AF.Exp, [stb], [stb], scale=-0.5)
    P.v(lambda e: e.tensor_scalar(out=nb, in0=mv[:, 0:1], scalar1=rs, scalar2=-1.0, op0=ALU.mult, op1=ALU.mult),
        [stb], [stb])
    P.act(y, y, AF.Identity, [stb, C.xb[j]], [C.xb[j]], bias=nb, scale=rs)
    P.g(lambda e: e.tensor_tensor(out=y, in0=y, in1=gain, op=ALU.mult), [C.xb[j], gb], [C.xb[j]])
    P.g(lambda e: e.tensor_tensor(out=y, in0=y, in1=bias, op=ALU.add), [C.xb[j], gb], [C.xb[j]])


def _pch():
    ch = [("cq0", O_CQ, 128, None), ("cq1", O_CQ + 128, 128, None), ("ckv", O_CKV, 128, None),
          ("kpe", O_KPE, 32, "k")]
    for i in range(3):
        ch.append(("dk%d" % i, O_DK + i * 128, 128, "r"))
    for i in range(3):
        ch.append(("dq%d" % i, O_DQ + i * 128, 128, "r"))
    for i in range(3):
        ch.append(("nq%d" % i, O_NQ + i * 128, 128, "r"))
    ch += [("kc", O_KC, 128, None), ("vc", O_VC, 128, None), ("ksl", O_KSL, 128, "r"), ("kw", O_KW, 128, "r")]
    return ch


PCH = _pch()
NPM = sum(c[2] for c in PCH)
NTM = 658


def phase_proj(C, d):
    P, A, ps, psb = C.P, C.A, C.ps, C.psb
    m = A.mark()
    bank = Rot(list(range(8)))
    tabC = A.f32(TOK)
    tabS = A.f32(TOK)
    tabCk = A.f32(TOK)
    tabSk = A.f32(TOK)
    tb = Buf("tab")
    tkb = Buf("tabk")
    P.dma("sync", tabC, d["ropeC64"][:, :], w=[tb])
    P.dma("sync", tabS, d["ropeS64"][:, :], w=[tb])
    P.dma("sync", tabCk[0:32, :], d["ropeCk"][:, :], w=[tkb])
    P.dma("sync", tabSk[0:32, :], d["ropeSk"][:, :], w=[tkb])
    cqT = A.bf(2 * TOK).rearrange("p (k t) -> p k t", t=TOK)
    cqb = Buf("cqT")
    ckvT = A.bf(TOK)
    ckvb = Buf("ckvT")
    w_rot = Rot([(A.bf(KC * 128).rearrange("p (k c) -> p k c", c=128),
                  A.bf(KC * 128).rearrange("p (k c) -> p k c", c=128), Buf()) for _ in range(2)])
    out_rot = Rot([(A.bf(TOK), Buf()) for _ in range(3)])
    t_rot = Rot([(A.f32(512), A.f32(512), Buf()) for _ in range(2)])
    Q, XK, XV, GT = d["Q"], d["XK"], d["XV"], d["gates"]
    off = 0
    import os
    _nch = int(os.environ.get("PROJ_NCH", "99"))
    for _ci, (name, src, n, rope) in enumerate(PCH):
        if _ci >= _nch:
            break
        wm, wp, wb = w_rot.next()
        P.dma("gpsimd", wm[:, :, 0:n], d["wpm"][:, off:off + n].rearrange("(k p) c -> p k c", p=128), w=[wb])
        if rope:
            P.dma("gpsimd", wp[:, :, 0:n], d["wpp"][:, off:off + n].rearrange("(k p) c -> p k c", p=128), w=[wb])
        raw_dst = None
        if name == "cq0":
            dst, dstb = cqT[:, 0, :], cqb
        elif name == "cq1":
            dst, dstb = cqT[:, 1, :], cqb
        elif name == "ckv":
            dst, dstb = ckvT, ckvb
        else:
            dst, dstb = out_rot.next()
            if name.startswith("nq"):
                raw_dst, raw_b = out_rot.next()
        tC, tS, tbb = (tabCk, tabSk, tkb) if rope == "k" else (tabC, tabS, tb)
        for tg in range(4):
            ts = slice(tg * 512, (tg + 1) * 512)
            bA = bank.next()
            for k in range(KC):
                P.mm(ps[bA][0:n, :], wm[:, k, 0:n], C.xT[:, k, ts], k == 0, k == KC - 1,
                     [wb, C.xTb[tg]], [psb[bA]])
            if rope:
                bB = bank.next()
                for k in range(KC):
                    P.mm(ps[bB][0:n, :], wp[:, k, 0:n], C.xT[:, k, ts], k == 0, k == KC - 1,
                         [wb, C.xTb[tg]], [psb[bB]])
                t1, t2, ttb = t_rot.next()
                P.v(lambda e, o=t1[0:n, :], a=ps[bA][0:n, :], b=tC[0:n, ts]: e.tensor_tensor(
                    out=o, in0=a, in1=b, op=ALU.mult), [psb[bA], tbb], [ttb])
                P.v(lambda e, o=t2[0:n, :], a=ps[bB][0:n, :], b=tS[0:n, ts]: e.tensor_tensor(
                    out=o, in0=a, in1=b, op=ALU.mult), [psb[bB], tbb], [ttb])
                if raw_dst is not None:
                    P.act(raw_dst[0:n, ts], ps[bA][0:n, :], AF.Copy, [psb[bA]], [raw_b])
                P.g(lambda e, o=dst[0:n, ts], a=t1[0:n, :], b=t2[0:n, :]: e.tensor_tensor(
                    out=o, in0=a, in1=b, op=ALU.add), [ttb], [dstb])
            else:
                P.act(dst[0:n, ts], ps[bA][0:n, :], AF.Copy, [psb[bA]], [dstb])
        if name == "kpe":
            for h in range(4):
                P.dma("sync", XK[XK_MLA + h * 96 + 64:XK_MLA + h * 96 + 96, :], dst[0:32, :], r=[dstb])
        elif name.startswith("dk"):
            i = int(name[2])
            P.dma("sync", XK[XK_DIL + i * 128:XK_DIL + (i + 1) * 128, :], dst, r=[dstb])
        elif name.startswith("dq"):
            i = int(name[2])
            P.dma("sync", Q[Q_DIL + i * 128:Q_DIL + (i + 1) * 128, :], dst, r=[dstb])
        elif name.startswith("nq"):
            i = int(name[2])
            P.dma("sync", Q[Q_NROT + i * 128:Q_NROT + (i + 1) * 128, :], dst, r=[dstb])
            P.dma("sync", Q[Q_NRAW + i * 128:Q_NRAW + (i + 1) * 128, :], raw_dst, r=[raw_b])
        elif name == "kc":
            P.dma("sync", XK[XK_KC:XK_KC + 128, :], dst, r=[dstb])
        elif name == "vc":
            P.dma("sync", XK[XK_VC:XK_VC + 128, :], dst, r=[dstb])
        elif name == "ksl":
            P.dma("sync", XK[XK_KS:XK_KS + 128, :], dst, r=[dstb])
        elif name == "kw":
            P.dma("sync", XK[XK_KW:XK_KW + 128, :], dst, r=[dstb])
        off += n

    import os
    _stop = os.environ.get("PROJ_STOP", "")
    if _stop == "chunks":
        P.barrier(); A.reset(m); return
    P.dma("sync", tabC[0:96, :], d["ropeCm"][:, :], w=[tb])
    P.dma("sync", tabS[0:96, :], d["ropeSm"][:, :], w=[tb])
    stage = A.f32(512)
    stb = Buf("stage")
    qn = A.f32(2)
    kvn = A.f32(1)
    nb = Buf("norms")
    for k in range(2):
        P.dma("sync", qn[:, k:k + 1], d["qn"][k * 128:(k + 1) * 128].rearrange("(p o) -> p o", o=1), w=[nb])
    P.dma("sync", kvn, d["kvn"].rearrange("(p o) -> p o", o=1), w=[nb])
    wuq_m = A.bf(2 * 384).rearrange("p (k c) -> p k c", c=384)
    wuq_p = A.bf(2 * 384).rearrange("p (k c) -> p k c", c=384)
    wukv_k = A.bf(256)
    wukv_v = A.bf(256)
    mwb = Buf("mlaw")
    for (dst_w, src_w) in ((wuq_m, d["wuq_m"]), (wuq_p, d["wuq_p"])):
        for k in range(2):
            P.dma("sync", stage[:, 0:384], src_w[k * 128:(k + 1) * 128, :], w=[stb])
            P.v(lambda e, o=dst_w[:, k, :], s=qn[:, k:k + 1]: e.tensor_scalar(
                out=o, in0=stage[:, 0:384], scalar1=s, scalar2=16.0, op0=ALU.mult, op1=ALU.mult),
                [stb, nb], [mwb])
    for (dst_w, src_w) in ((wukv_k, d["wukv_k"]), (wukv_v, d["wukv_v"])):
        P.dma("sync", stage[:, 0:256], src_w[:, :], w=[stb])
        P.v(lambda e, o=dst_w: e.tensor_scalar(
            out=o, in0=stage[:, 0:256], scalar1=kvn[:, 0:1], scalar2=math.sqrt(128.0), op0=ALU.mult, op1=ALU.mult),
            [stb, nb], [mwb])
    wtm = A.bf(KC * NTM).rearrange("p (k c) -> p k c", c=NTM)
    wtb = Buf("wtm")
    P.dma("gpsimd", wtm[:, :, 0:384], d["wtm"][:, 0:384].rearrange("(k p) c -> p k c", p=128), w=[wtb])
    P.dma("gpsimd", wtm[:, :, 384:NTM], d["wtm"][:, 384:NTM].rearrange("(k p) c -> p k c", p=128), w=[wtb])
    xv_rot = Rot([(A.bf(XV_COLS), Buf()) for _ in range(2)])
    for xv, xvb in xv_rot.items:
        P.g(lambda e, o=xv: e.memset(o, 1.0), [], [xvb])
    gt_rot = Rot([(A.f32(18), Buf()) for _ in range(2)])
    sq = A.bf(1024).rearrange("p (k t) -> p k t", t=512)
    sqb = Buf("sq")
    rq = A.f32(512)
    rkv = A.f32(512)
    rb = Buf("rq")
    rkb = Buf("rkv")
    ckvn = A.bf(512)
    cnb = Buf("ckvn")
    qo_rot = Rot([(A.bf(512), Buf()) for _ in range(2)])
    if _stop == "mlaw":
        P.barrier(); A.reset(m); return
    for tg in range(4):
        ts = slice(tg * 512, (tg + 1) * 512)
        for k in range(2):
            P.v(lambda e, o=sq[:, k, :], a=cqT[:, k, ts]: e.tensor_tensor(out=o, in0=a, in1=a, op=ALU.mult),
                [cqb], [sqb])
        b = bank.next()
        for k in range(2):
            P.mm(ps[b][:], C.ones_bf, sq[:, k, :], k == 0, k == 1, [sqb, C.cb], [psb[b]])
        P.act(rq, ps[b][:], AF.Ln, [psb[b], C.cb], [rb], bias=C.eps_q)
        P.act(rq, rq, AF.Exp, [rb], [rb], scale=-0.5)
        P.v(lambda e, o=sq[:, 0, :], a=ckvT[:, ts]: e.tensor_tensor(out=o, in0=a, in1=a, op=ALU.mult),
            [ckvb], [sqb])
        b = bank.next()
        P.mm(ps[b][:], C.ones_bf, sq[:, 0, :], True, True, [sqb, C.cb], [psb[b]])
        P.act(rkv, ps[b][:], AF.Ln, [psb[b], C.cb], [rkb], bias=C.eps_kv)
        P.act(rkv, rkv, AF.Exp, [rkb], [rkb], scale=-0.5)
        P.v(lambda e, a=ckvT[:, ts]: e.tensor_tensor(out=ckvn, in0=a, in1=rkv, op=ALU.mult),
            [ckvb, rkb], [cnb])
        if _stop == "stats":
            continue
        for h in range(4):
            bA = bank.next()
            bB = bank.next()
            for k in range(2):
                P.mm(ps[bA][0:96, :], wuq_m[:, k, h * 96:(h + 1) * 96], cqT[:, k, ts], k == 0, k == 1,
                     [mwb, cqb], [psb[bA]])
            for k in range(2):
                P.mm(ps[bB][0:96, :], wuq_p[:, k, h * 96:(h + 1) * 96], cqT[:, k, ts], k == 0, k == 1,
                     [mwb, cqb], [psb[bB]])
            t1, t2, ttb = t_rot.next()
            P.v(lambda e, o=t1[0:96, :], a=ps[bA][0:96, :], c=tabC[0:96, ts]: e.tensor_tensor(
                out=o, in0=a, in1=c, op=ALU.mult), [psb[bA], tb], [ttb])
            P.v(lambda e, o=t2[0:96, :], a=ps[bB][0:96, :], c=tabS[0:96, ts]: e.tensor_tensor(
                out=o, in0=a, in1=c, op=ALU.mult), [psb[bB], tb], [ttb])
            P.g(lambda e, o=t1[0:96, :], a=t1[0:96, :], c=t2[0:96, :]: e.tensor_tensor(
                out=o, in0=a, in1=c, op=ALU.add), [ttb], [ttb])
            qo, qob = qo_rot.next()
            P.g(lambda e, o=qo[0:96, :], a=t1[0:96, :], c=rq[0:96, :]: e.tensor_tensor(
                out=o, in0=a, in1=c, op=ALU.mult), [ttb, rb], [qob])
            P.dma("sync", Q[Q_MLA + h * 96:Q_MLA + (h + 1) * 96, ts], qo[0:96, :], r=[qob])
        for c2 in range(2):
            b = bank.next()
            P.mm(ps[b][:], wukv_k[:, c2 * 128:(c2 + 1) * 128], ckvn, True, True, [mwb, cnb], [psb[b]])
            qo, qob = qo_rot.next()
            P.act(qo, ps[b][:], AF.Copy, [psb[b]], [qob])
            for hh in range(2):
                h = c2 * 2 + hh
                P.dma("sync", XK[XK_MLA + h * 96:XK_MLA + h * 96 + 64, ts], qo[hh * 64:(hh + 1) * 64, :], r=[qob])
        if _stop == "qk":
            continue
        for jj in range(4):
            j = tg * 4 + jj
            tsl = slice(j * 128, (j + 1) * 128)
            bX = bank.next()
            bY = bank.next()
            bZ = bank.next()
            for k in range(KC):
                P.mm(ps[bX][:], C.xT[:, k, tsl], wtm[:, k, 0:512], k == 0, k == KC - 1,
                     [wtb, C.xTb[tg]], [psb[bX]])
            for k in range(KC):
                P.mm(ps[bY][:, 0:146], C.xT[:, k, tsl], wtm[:, k, 512:NTM], k == 0, k == KC - 1,
                     [wtb, C.xTb[tg]], [psb[bY]])
            P.mm(ps[bZ][:, 0:256], ckvn[:, jj * 128:(jj + 1) * 128], wukv_v, True, True, [cnb, mwb], [psb[bZ]])
            xv, xvb = xv_rot.next()
            P.act(xv[:, XV_DIL:XV_DIL + 390].rearrange("p (h c) -> p h c", c=65)[:, :, 0:64],
                  ps[bX][:, 0:384].rearrange("p (h c) -> p h c", c=64), AF.Copy, [psb[bX]], [xvb])
            P.v(lambda e, o=xv[:, XV_NSA:XV_NSA + 130].rearrange("p (h c) -> p h c", c=65)[:, :, 0:64],
                a=ps[bX][:, 384:512].rearrange("p (h c) -> p h c", c=64): e.tensor_copy(out=o, in_=a),
                [psb[bX]], [xvb])
            P.v(lambda e, o=xv[:, XV_NSA + 130:XV_NSA + 260].rearrange("p (h c) -> p h c", c=65)[:, :, 0:64],
                a=ps[bY][:, 0:128].rearrange("p (h c) -> p h c", c=64): e.tensor_copy(out=o, in_=a),
                [psb[bY]], [xvb])
            P.act(xv[:, XV_MLA:XV_MLA + 260].rearrange("p (h c) -> p h c", c=65)[:, :, 0:64],
                  ps[bZ][:, 0:256].rearrange("p (h c) -> p h c", c=64), AF.Copy, [psb[bZ]], [xvb])
            gt, gtb = gt_rot.next()
            P.act(gt, ps[bY][:, 128:146], AF.Exp, [psb[bY]], [gtb], scale=-1.0)
            P.v(lambda e, o=gt: e.tensor_scalar(out=o, in0=o, scalar1=1.0, scalar2=None, op0=ALU.add), [gtb], [gtb])
            P.v(lambda e, o=gt: e.reciprocal(out=o, in_=o), [gtb], [gtb])
            P.dma("sync", GT[tsl, :], gt, r=[gtb])
            P.dma("sync", XV[tsl, :], xv, r=[xvb])
    P.barrier()
    A.reset(m)


def _bf(a):
    return np.ascontiguousarray(a).astype(ml_dtypes.bfloat16)


def own_positions(r):
    j = np.arange(NT)[:, None]
    p = np.arange(128)[None, :]
    return ((4 * j + r) * 128 + p).reshape(-1)


def host_rope_tables(r):
    pos = own_positions(r).astype(np.float32)
    inv64 = (10000.0 ** (-np.arange(0, 64, 2, dtype=np.float32) / np.float32(64))).astype(np.float32)
    ang = (pos[:, None] * inv64[None, :]).astype(np.float32)
    c, s = np.cos(ang).astype(np.float32), np.sin(ang).astype(np.float32)
    C64 = np.concatenate([c, c], axis=1).T
    S64 = np.concatenate([-s, s], axis=1).T
    C64 = np.concatenate([C64, C64], axis=0)
    S64 = np.concatenate([S64, S64], axis=0)
    inv32 = (10000.0 ** (-np.arange(0, 32, 2, dtype=np.float32) / np.float32(32))).astype(np.float32)
    angm = (pos[:, None] * inv32[None, :]).astype(np.float32)
    cm, sm = np.cos(angm).astype(np.float32), np.sin(angm).astype(np.float32)
    Ck = np.concatenate([cm, cm], axis=1).T
    Sk = np.concatenate([-sm, sm], axis=1).T
    Cm = np.concatenate([np.ones((64, TOK), np.float32), Ck], axis=0)
    Sm = np.concatenate([np.zeros((64, TOK), np.float32), Sk], axis=0)
    f = lambda a: np.ascontiguousarray(a, dtype=np.float32)
    return {"ropeC64": f(C64), "ropeS64": f(S64), "ropeCm": f(Cm), "ropeSm": f(Sm), "ropeCk": f(Ck), "ropeSk": f(Sk)}


def host_proj_weights(inputs, l, sfx):
    w = inputs["w_in"][l]
    main, perm = [], []
    for (name, src, n, rope) in PCH:
        blk = w[:, src:src + n]
        if name.startswith("nq"):
            i = int(name[2])
            blk = np.concatenate([w[:, O_NQ + i * 64:O_NQ + (i + 1) * 64],
                                  w[:, O_NQ + (3 + i) * 64:O_NQ + (4 + i) * 64]], axis=1)
        main.append(blk)
        if rope == "r":
            idx = np.arange(n)
            idx = (idx // 64) * 64 + ((idx % 64) + 32) % 64
            perm.append(blk[:, idx])
        elif rope == "k":
            idx = (np.arange(n) + 16) % 32
            perm.append(blk[:, idx])
        else:
            perm.append(blk)
    wuq = inputs["mla_w_uq"][l]
    idx = np.arange(384)
    hd, dd = idx // 96, idx % 96
    pidx = np.where(dd < 64, idx, hd * 96 + 64 + ((dd - 64) + 16) % 32)
    wukv = inputs["mla_w_ukv"][l].reshape(128, 4, 128)
    c = np.ascontiguousarray
    return {
        "wpm" + sfx: c(np.concatenate(main, axis=1)), "wpp" + sfx: c(np.concatenate(perm, axis=1)),
        "wtm" + sfx: c(np.concatenate([w[:, O_DV:O_DV + 384], w[:, O_VSL:O_VSL + 128], w[:, O_VW:O_VW + 128],
                                       w[:, O_GL:O_GL + 18]], axis=1)),
        "wuq_m" + sfx: c(wuq), "wuq_p" + sfx: c(wuq[:, pidx]),
        "wukv_k" + sfx: c(wukv[:, :, :64].reshape(128, 256)), "wukv_v" + sfx: c(wukv[:, :, 64:].reshape(128, 256)),
        "qn" + sfx: c(inputs["mla_q_norm"][l]), "kvn" + sfx: c(inputs["mla_kv_norm"][l]),
    }


def own_rows(x, c):
    b, r = c // 4, c % 4
    sh = x.shape[2:]
    return np.ascontiguousarray(x[b].reshape((NT, NR, 128) + sh)[:, r].reshape((TOK,) + sh))


def host_consts():
    return {"c_ident_bf": _bf(np.eye(128)), "c_ident_f": np.eye(128, dtype=np.float32)}


def decl_proj_inputs(B, sfx):
    d = {}
    d["wpm"] = B.inp("wpm" + sfx, [D_MODEL, NPM])
    d["wpp"] = B.inp("wpp" + sfx, [D_MODEL, NPM])
    d["wtm"] = B.inp("wtm" + sfx, [D_MODEL, NTM])
    d["wuq_m"] = B.inp("wuq_m" + sfx, [256, 384])
    d["wuq_p"] = B.inp("wuq_p" + sfx, [256, 384])
    d["wukv_k"] = B.inp("wukv_k" + sfx, [128, 256])
    d["wukv_v"] = B.inp("wukv_v" + sfx, [128, 256])
    d["qn"] = B.inp("qn" + sfx, [256])
    d["kvn"] = B.inp("kvn" + sfx, [128])
    return d


def decl_rope_inputs(B, d):
    for nm, rows in (("ropeC64", 128), ("ropeS64", 128), ("ropeCm", 96), ("ropeSm", 96), ("ropeCk", 32), ("ropeSk", 32)):
        if nm not in B.din:
            B.inp(nm, [rows, TOK])
        d[nm] = B.din[nm]


SC64 = 64.0 ** -0.5
SC96 = 96.0 ** -0.5


def host_attn_tables(r):
    k = np.arange(128)[:, None]
    q = np.arange(128)[None, :]
    causal = np.where(k <= q, 0.0, NEG).astype(np.float32)
    zeros = np.zeros((128, 128), np.float32)
    negs = np.full((128, 128), NEG, np.float32)
    DB = np.stack([zeros if i < r else (causal if i == r else negs) for i in range(4)], axis=1)
    DB3 = np.repeat(DB[:, :, None, :], 3, axis=2)
    hi = np.zeros((128, 20, 128), np.float32)
    lo = np.zeros((128, 8, 128), np.float32)
    for ii in range(20):
        i = ii - 16
        delta = r - i
        if 0 <= delta <= 16:
            dist = q + delta * 128 - k
            m = ((dist >= 0) & (dist <= 128)).astype(np.float64) + \
                ((dist >= 0) & (dist <= 512) & (dist % 4 == 0)) + ((dist >= 0) & (dist <= 2048) & (dist % 16 == 0))
            val = np.where(m > 0, np.log(np.maximum(m, 1.0)) / SC64, NEG)
        else:
            val = np.full((128, 128), NEG, np.float64)
        h = val.astype(np.float32).astype(ml_dtypes.bfloat16).astype(np.float64)
        hi[:, ii, :] = h
        if ii >= 12:
            lo[:, ii - 12, :] = np.where(val > NEG / 2, val - h, 0.0)
    CMPB = np.zeros((128, 4, 2, 3, 128), np.float32)
    for jm in range(4):
        gm = 4 * jm + r
        for pl in range(2):
            crel = k + (128 * (pl - 1))
            vis = (16 * crel + 31) <= (128 * gm + q)
            CMPB[:, jm, pl, :, :] = np.where(vis, 0.0, NEG)[:, None, :]
    WINB = np.zeros((128, 8, 3, 128), np.float32)
    for ii in range(8):
        i = ii - 4
        delta = r - i
        if delta < 0 or delta > 4:
            t = negs
        elif delta == 0:
            t = causal
        elif delta == 4:
            t = np.where(k >= q, 0.0, NEG).astype(np.float32)
        else:
            t = zeros
        WINB[:, ii, :, :] = t[:, None, :]
    SELB = np.zeros((128, NT, 128), np.float32)
    blk = np.arange(128)[None, :]
    for j in range(NT):
        qpos = 128 * (4 * j + r) + np.arange(128)[:, None]
        cur = qpos // 64
        forced = (blk == 0) | (blk == cur) | (blk == cur - 1)
        SELB[:, j, :] = np.where(forced, 1e4, np.where(blk <= cur, 0.0, -1e4))
    return {"tDB": _bf(DB.reshape(128, 512)), "tDB3": _bf(DB3.reshape(128, 4 * 384)),
            "tDILhi": _bf(hi.reshape(128, 20 * 128)), "tDILlo": _bf(lo.reshape(128, 8 * 128)),
            "tCMPB": _bf(CMPB.reshape(128, 4 * 2 * 384)), "tWINB": _bf(WINB.reshape(128, 8 * 384)),
            "tSELB": np.ascontiguousarray(SELB.reshape(128, NT * 128))}


def host_attn_consts():
    c = np.arange(512)[:, None]
    jb = np.arange(128)[None, :]
    ov = ((c >= 4 * jb - 1) & (c <= 4 * jb + 3) & (c <= 510)).astype(np.float32)
    ovl = np.zeros((128, 4, 129), np.float32)
    ovl[:, :, 0] = 1.0
    ovl[:, :, 1:] = ov.reshape(4, 128, 128).transpose(1, 0, 2)
    et = (np.arange(8192)[None, :] // 64 == np.arange(128)[:, None]).astype(np.float32)
    return {"tOVL": _bf(ovl.reshape(128, 4 * 129)), "tETAB": _bf(et)}


ATT_TABLES = (("tDB", 512), ("tDB3", 1536), ("tDILhi", 2560), ("tDILlo", 1024), ("tCMPB", 3072), ("tWINB", 3072),
              ("tOVL", 516), ("tETAB", 8192))


def decl_attn_inputs(B, sfx):
    d = {}
    for nm, cols in ATT_TABLES:
        if nm not in B.din:
            B.inp(nm, [128, cols], BF16)
        d[nm] = B.din[nm]
    if "tSELB" not in B.din:
        B.inp("tSELB", [128, NT * 128], F32)
    d["tSELB"] = B.din["tSELB"]
    d["w_out"] = B.inp("w_out" + sfx, [1024, 1024])
    d["cmp_pos"] = B.inp("cmp_pos" + sfx, [2, 32, 64])
    d["cmp_w1"] = B.inp("cmp_w1" + sfx, [2, 2048, 256])
    d["cmp_w2"] = B.inp("cmp_w2" + sfx, [2, 256, 64])
    return d


def host_attn_weights(inputs, l, sfx):
    c = np.ascontiguousarray
    return {"w_out" + sfx: c(inputs["w_out"][l]), "cmp_pos" + sfx: c(inputs["nsa_cmp_pos"][l]),
            "cmp_w1" + sfx: c(inputs["nsa_cmp_w1"][l]), "cmp_w2" + sfx: c(inputs["nsa_cmp_w2"][l])}


class AttCtx:
    pass


def att_common(C):
    T = AttCtx()
    T.s_rot = Rot([0, 1, 2])
    T.a_rot = Rot([3, 4, 5])
    return T


def finalize_head(C, acc_ap, accb, o_dst, ob, sm_rot, gate=None, gb=None, first=True, fdst=None, fb=None,
                  skip=False):
    P = C.P
    sm, smb = sm_rot.next()
    P.v(lambda e: e.tensor_scalar(out=sm[:, 0:1], in0=acc_ap[:, 64:65], scalar1=1e-30, scalar2=None, op0=ALU.max),
        [accb], [smb])
    P.v(lambda e: e.reciprocal(out=sm[:, 0:1], in_=sm[:, 0:1]), [smb], [smb])
    if gate is None:
        P.v(lambda e: e.tensor_scalar(out=o_dst, in0=acc_ap[:, 0:64], scalar1=sm[:, 0:1], scalar2=None, op0=ALU.mult),
            [accb, smb], [ob])
        return sm, smb
    P.v(lambda e: e.tensor_tensor(out=sm[:, 1:2], in0=sm[:, 0:1], in1=gate, op=ALU.mult), [smb, gb], [smb])
    if skip:
        return sm, smb
    if first:
        P.v(lambda e: e.tensor_scalar(out=fdst, in0=acc_ap[:, 0:64], scalar1=sm[:, 1:2], scalar2=None, op0=ALU.mult),
            [accb, smb], [fb])
    else:
        P.v(lambda e: e.scalar_tensor_tensor(out=fdst, in0=acc_ap[:, 0:64], scalar=sm[:, 1:2], in1=fdst,
                                             op0=ALU.mult, op1=ALU.add), [accb, smb, fb], [fb])
    return sm, smb


def out_project(C, O, ob, W, wo_d, row0, first):
    P, A, ps, psb = C.P, C.A, C.ps, C.psb
    m = A.mark()
    nk = W // 128
    wo = A.bf(nk * D_MODEL).rearrange("p (k d) -> p k d", d=D_MODEL)
    wob = Buf("wo_mix")
    for k in range(nk):
        P.dma("gpsimd", wo[:, k, :], wo_d[row0 + k * 128:row0 + (k + 1) * 128, :], w=[wob])
    ot_rot = Rot([(A.bf(nk * 128).rearrange("p (k t) -> p k t", t=128), Buf()) for _ in range(2)])
    pb = Rot([7, 0, 1, 2])
    for j in range(NT):
        pv = psbf(C, 6)
        for k in range(nk):
            P.tr(pv[:, k * 128:(k + 1) * 128], O[:, j, k * 128:(k + 1) * 128], C.ident_bf, [ob, C.cb], [psb[6]])
        ot, otb = ot_rot.next()
        P.v(lambda e, o=ot, a=pv[:, 0:nk * 128].rearrange("p (k t) -> p k t", t=128): e.tensor_copy(out=o, in_=a),
            [psb[6]], [otb])
        for nh in range(2):
            b = pb.next()
            cs = slice(nh * 512, (nh + 1) * 512)
            for k in range(nk):
                P.mm(ps[b][:], ot[:, k, :], wo[:, k, cs], k == 0, k == nk - 1, [otb, wob], [psb[b]])
            xs = C.x_tok[:, j, cs]
            po = ps[b][:]
            if first:
                P.v(lambda e, x=xs, p=po: e.scalar_tensor_tensor(
                    out=x, in0=x, scalar=ALPHA, in1=p, op0=ALU.mult, op1=ALU.add), [psb[b]], [C.xb[j]])
            else:
                P.v(lambda e, x=xs, p=po: e.tensor_tensor(out=x, in0=x, in1=p, op=ALU.add), [psb[b]], [C.xb[j]])
    P.barrier()
    A.reset(m)


def phase_mla(C, d):
    P, A, ps, psb = C.P, C.A, C.ps, C.psb
    T = att_common(C)
    m = A.mark()
    XK, XV, Q = d["XKall"], d["XVall"], d["Q"]
    Vm = A.bf(NR * NT * 260).rearrange("p (r j c) -> p r j c", r=NR, j=NT)
    vb = Buf("Vm")
    for r in range(NR):
        P.dma("sync", Vm[:, r, :, :], XV[r, :, XV_MLA:XV_MLA + 260].rearrange("(j p) c -> p j c", p=128), w=[vb])
    tDB = A.bf(512)
    tbb = Buf("tDB")
    P.dma("sync", tDB, d["tDB"][:, :], w=[tbb])
    kt_rot = Rot([(A.bf(NR * TOK).rearrange("p (r t) -> p r t", t=TOK), Buf()) for _ in range(2)])
    q_rot = Rot([(A.bf(TOK), Buf()) for _ in range(2)])
    pt_rot = Rot([(A.bf(512), Buf()) for _ in range(3)])
    sm_rot = Rot([(A.f32(2), Buf()) for _ in range(4)])
    O = A.bf(NT * 256).rearrange("p (j c) -> p j c", c=256)
    ob = Buf("O_mla")
    for h in range(4):
        KT, ktb = kt_rot.next()
        for r in range(NR):
            P.dma("sync", KT[0:96, r, :], XK[r, XK_MLA + h * 96:XK_MLA + (h + 1) * 96, :], w=[ktb])
        QT, qtb = q_rot.next()
        P.dma("sync", QT[0:96, :], Q[Q_MLA + h * 96:Q_MLA + (h + 1) * 96, :], w=[qtb])
        for j in range(NT):
            acc = T.a_rot.next()
            nb = j + 1
            for b in range(nb):
                sb = T.s_rot.next()
                last = (b == nb - 1)
                if last:
                    P.mm(ps[sb][:], C.ident_bf, tDB, True, False, [C.cb, tbb], [psb[sb]])
                for i in range(4):
                    kb = 4 * b + i
                    P.mm(ps[sb][:, i * 128:(i + 1) * 128], KT[0:96, kb % 4, (kb // 4) * 128:(kb // 4 + 1) * 128],
                         QT[0:96, j * 128:(j + 1) * 128], (not last) and i == 0, i == 3, [ktb, qtb], [psb[sb]])
                pt, ptb = pt_rot.next()
                P.act(pt, ps[sb][:], AF.Exp, [psb[sb]], [ptb], scale=SC96)
                for i in range(4):
                    kb = 4 * b + i
                    P.mm(ps[acc][:, 0:65], pt[:, i * 128:(i + 1) * 128],
                         Vm[:, kb % 4, kb // 4, h * 65:(h + 1) * 65], b == 0 and i == 0, last and i == 3,
                         [ptb, vb], [psb[acc]])
            finalize_head(C, ps[acc], psb[acc], O[:, j, h * 64:(h + 1) * 64], ob, sm_rot)
    if "dbg_mla" in d:
        P.dma("sync", d["dbg_mla"][:, :], O.rearrange("p j c -> p (j c)"), r=[ob])
    P.barrier()
    out_project(C, O, ob, 256, d["w_out"], 0, True)
    A.reset(m)


def phase_dil(C, d):
    P, A, ps, psb = C.P, C.A, C.ps, C.psb
    T = att_common(C)
    m = A.mark()
    XK, XV, Q = d["XKall"], d["XVall"], d["Q"]
    thi = A.bf(2560)
    tlo = A.bf(1024)
    tbb = Buf("tDIL")
    P.dma("sync", thi, d["tDILhi"][:, :], w=[tbb])
    P.dma("sync", tlo, d["tDILlo"][:, :], w=[tbb])
    kt_rot = Rot([(A.bf(NR * TOK).rearrange("p (r t) -> p r t", t=TOK), Buf()) for _ in range(2)])
    v_rot = Rot([(A.bf(NR * NT * 130).rearrange("p (r j c) -> p r j c", r=NR, j=NT), Buf()) for _ in range(2)])
    q_rot = Rot([(A.bf(TOK), Buf()) for _ in range(2)])
    pt_rot = Rot([(A.bf(512), Buf()) for _ in range(3)])
    sm_rot = Rot([(A.f32(2), Buf()) for _ in range(4)])
    O = A.bf(NT * 384).rearrange("p (j c) -> p j c", c=384)
    ob = Buf("O_dil")
    for c2 in range(3):
        KT, ktb = kt_rot.next()
        Vd, vb = v_rot.next()
        for r in range(NR):
            P.dma("sync", KT[:, r, :], XK[r, XK_DIL + c2 * 128:XK_DIL + (c2 + 1) * 128, :], w=[ktb])
            P.dma("sync", Vd[:, r, :, :],
                  XV[r, :, XV_DIL + c2 * 130:XV_DIL + (c2 + 1) * 130].rearrange("(j p) c -> p j c", p=128), w=[vb])
        QT, qtb = q_rot.next()
        P.dma("sync", QT, Q[Q_DIL + c2 * 128:Q_DIL + (c2 + 1) * 128, :], w=[qtb])
        for hh in range(2):
            h = c2 * 2 + hh
            pr = slice(hh * 64, (hh + 1) * 64)
            for j in range(NT):
                acc = T.a_rot.next()
                bbs = [bb for bb in range(5) if 4 * j - 16 + 4 * bb >= 0]
                for bi, bb in enumerate(bbs):
                    sb = T.s_rot.next()
                    P.mm(ps[sb][:], C.ident_bf, thi[:, bb * 512:(bb + 1) * 512], True, False, [C.cb, tbb], [psb[sb]])
                    if bb >= 3:
                        P.mm(ps[sb][:], C.ident_bf, tlo[:, (bb - 3) * 512:(bb - 2) * 512], False, False,
                             [C.cb, tbb], [psb[sb]])
                    for i in range(4):
                        kb = 4 * j - 16 + 4 * bb + i
                        P.mm(ps[sb][:, i * 128:(i + 1) * 128], KT[pr, kb % 4, (kb // 4) * 128:(kb // 4 + 1) * 128],
                             QT[pr, j * 128:(j + 1) * 128], False, i == 3, [ktb, qtb], [psb[sb]])
                    pt, ptb = pt_rot.next()
                    P.act(pt, ps[sb][:], AF.Exp, [psb[sb]], [ptb], scale=SC64)
                    for i in range(4):
                        kb = 4 * j - 16 + 4 * bb + i
                        P.mm(ps[acc][:, 0:65], pt[:, i * 128:(i + 1) * 128],
                             Vd[:, kb % 4, kb // 4, hh * 65:(hh + 1) * 65], bi == 0 and i == 0,
                             bi == len(bbs) - 1 and i == 3, [ptb, vb], [psb[acc]])
                finalize_head(C, ps[acc], psb[acc], O[:, j, h * 64:(h + 1) * 64], ob, sm_rot)
    if "dbg_dil" in d:
        P.dma("sync", d["dbg_dil"][:, :], O.rearrange("p j c -> p (j c)"), r=[ob])
    P.barrier()
    out_project(C, O, ob, 384, d["w_out"], 256, False)
    A.reset(m)


def phase_nsa(C, d):
    P, A, ps, psb = C.P, C.A, C.ps, C.psb
    T = att_common(C)
    m = A.mark()
    XK, XV, Q = d["XKall"], d["XVall"], d["Q"]
    kcT = A.bf(512)
    kcb = Buf("kcT")
    VCO = A.bf(4 * 2 * 193).rearrange("p (c g x) -> p c g x", c=4, g=2)
    vcob = Buf("VCO")
    P.v(lambda e: e.memset(kcT, 0.0), [], [kcb])
    for g in range(2):
        P.dma("sync", VCO[:, :, g, 64:193], d["tOVL"].rearrange("p (c x) -> p c x", x=129), w=[vcob])
    m2 = A.mark()
    KG = A.bf(SEQ)
    KGv = KG.rearrange("p (j r t) -> p j r t", j=NT, r=NR)
    kgb = Buf("KG")
    W1 = A.bf(32 * 256).rearrange("p (l h) -> p l h", h=256)
    w1b = Buf("W1")
    pos_sb = A.f32(128)
    posT = A.bf(32)
    posb = Buf("pos")
    biasH = A.f32(2)
    bhb = Buf("biasH")
    hid = A.bf(2 * 2 * 512).rearrange("p (g h c) -> p g h c", g=2, h=2)
    hidb = Buf("hid")
    w2pad = A.bf(2 * 2 * 128).rearrange("p (g h c) -> p g h c", g=2, h=2)
    w2v = A.bf(2 * 64).rearrange("p (h c) -> p h c", c=64)
    w2b = Buf("w2")
    P.g(lambda e: e.memset(w2pad, 0.0), [], [w2b])
    P.g(lambda e: e.memset(hid, 0.0), [], [hidb])
    for g in range(2):
        for hc in range(2):
            P.dma("gpsimd", w2pad[:, g, hc, g * 64:(g + 1) * 64], d["cmp_w2"][0, hc * 128:(hc + 1) * 128, :], w=[w2b])
    P.dma("gpsimd", w2v, d["cmp_w2"][1].rearrange("(h p) c -> p h c", p=128), w=[w2b])
    brot = Rot([7, 0, 1, 2])
    for kv in range(2):
        rows = XK_KC if kv == 0 else XK_VC
        for r in range(NR):
            P.dma("sync", KGv[:, :, r, :], XK[r, rows:rows + 128, :].rearrange("p (j t) -> p j t", t=128), w=[kgb])
        for half in range(2):
            P.dma("gpsimd", W1[half * 64:(half + 1) * 64, :, :],
                  d["cmp_w1"][kv].rearrange("(l q) h -> q l h", q=64), w=[w1b])
        for half in range(2):
            P.dma("sync", pos_sb[0:32, half * 64:(half + 1) * 64], d["cmp_pos"][kv], w=[posb])
        P.tr(ps[6][:, 0:32], pos_sb[0:32, :], C.ident_f[0:32, 0:32], [posb, C.cb], [psb[6]])
        P.v(lambda e: e.tensor_copy(out=posT, in_=ps[6][:, 0:32]), [psb[6]], [posb])
        b = brot.next()
        for hc in range(2):
            for l in range(32):
                P.mm(ps[b][:, hc:hc + 1], W1[0:64, l, hc * 128:(hc + 1) * 128], posT[0:64, l:l + 1],
                     hc == 0 and l == 0, l == 31, [w1b, posb], [psb[b]])
        P.v(lambda e, b=b: e.tensor_copy(out=biasH, in_=ps[b][:, 0:2]), [psb[b]], [bhb])
        for g in range(2):
            gp = slice(g * 64, (g + 1) * 64)
            for hc in range(2):
                b = brot.next()
                for l in range(32):
                    P.mm(ps[b][:, 0:511], W1[gp, l, hc * 128:(hc + 1) * 128], KG[gp, l:l + 16 * 510 + 1:16],
                         l == 0, l == 31, [w1b, kgb], [psb[b]])
                P.act(hid[:, g, hc, 0:511], ps[b][:, 0:511], AF.Silu, [psb[b], bhb], [hidb], bias=biasH[:, hc:hc + 1])
        if kv == 0:
            b = brot.next()
            n = 0
            for g in range(2):
                for hc in range(2):
                    P.mm(ps[b][:, 0:511], w2pad[:, g, hc, :], hid[:, g, hc, 0:511], n == 0, n == 3,
                         [w2b, hidb], [psb[b]])
                    n += 1
            P.act(kcT[:, 0:511], ps[b][:, 0:511], AF.Copy, [psb[b]], [kcb])
        else:
            for g in range(2):
                for cb in range(4):
                    b = brot.next()
                    for hc in range(2):
                        P.mm(ps[b][:, 0:64], hid[:, g, hc, cb * 128:(cb + 1) * 128], w2v[:, hc, :], hc == 0, hc == 1,
                             [hidb, w2b], [psb[b]])
                    P.v(lambda e, o=VCO[:, cb, g, 0:64], a=ps[b][:, 0:64]: e.tensor_copy(out=o, in_=a),
                        [psb[b]], [vcob])
    P.barrier()
    A.reset(m2)
    ETAB = C.xT_raw[:, 0:SEQ]
    ksT = C.xT_raw[:, SEQ:2 * SEQ].rearrange("p (r t) -> p r t", t=TOK)
    etb = Buf("ETAB")
    ksb = Buf("ksT")
    P.dma("sync", ETAB, d["tETAB"][:, :], w=[etb] + C.xTb)
    kwT = A.bf(NR * TOK).rearrange("p (r t) -> p r t", t=TOK)
    kwb = Buf("kwT")
    Vn = A.bf(NR * NT * 260).rearrange("p (r j c) -> p r j c", r=NR, j=NT)
    vnb = Buf("Vn")
    for r in range(NR):
        P.dma("sync", ksT[:, r, :], XK[r, XK_KS:XK_KS + 128, :], w=[ksb] + C.xTb)
        P.dma("sync", kwT[:, r, :], XK[r, XK_KW:XK_KW + 128, :], w=[kwb])
        P.dma("sync", Vn[:, r, :, :], XV[r, :, XV_NSA:XV_NSA + 260].rearrange("(j p) c -> p j c", p=128), w=[vnb])
    SELB = A.f32(NT * 128).rearrange("p (j c) -> p j c", c=128)
    tDB3 = A.bf(1536)
    tCMPB = A.bf(3072)
    tWINB = A.bf(3072)
    gts = A.f32(NT * 18).rearrange("p (j c) -> p j c", c=18)
    tb = Buf("nsa_tabs")
    P.dma("sync", SELB, d["tSELB"].rearrange("p (j c) -> p j c", c=128), w=[tb])
    P.dma("sync", tDB3, d["tDB3"][:, :], w=[tb])
    P.dma("sync", tCMPB, d["tCMPB"][:, :], w=[tb])
    P.dma("sync", tWINB, d["tWINB"][:, :], w=[tb])
    P.dma("sync", gts, d["gates"].rearrange("(j p) c -> p j c", p=128), w=[tb])
    O = A.bf(NT * 384).rearrange("p (j c) -> p j c", c=384)
    ob = Buf("O_nsa")
    qn_rot = Rot([(A.bf(6 * 128).rearrange("p (c t) -> p c t", t=128), Buf()) for _ in range(2)])
    pt_rot = Rot([(A.bf(384), Buf()) for _ in range(3)])
    sel_rot = Rot([(A.bf(384), Buf()) for _ in range(2)])
    oacc_rot = Rot([(A.f32(384), Buf()) for _ in range(2)])
    sm_rot = Rot([(A.f32(2), Buf()) for _ in range(6)])
    imp = A.f32(128)
    score = A.f32(128)
    sc2 = A.f32(128)
    selbf = A.f32(128)
    m8 = A.f32(16)
    swb = Buf("selwork")
    accs = [3, 4, 5]
    import os
    _br = os.environ.get("NSA_BR", "csw")
    _dbg = _br != "csw"
    if _dbg:
        P.v(lambda e: e.memset(gts, 1.0), [tb], [tb])
    for j in range(NT):
        qn, qnb = qn_rot.next()
        for t in range(3):
            P.dma("sync", qn[:, t, :], Q[Q_NRAW + t * 128:Q_NRAW + (t + 1) * 128, j * 128:(j + 1) * 128], w=[qnb])
            P.dma("sync", qn[:, 3 + t, :], Q[Q_NROT + t * 128:Q_NROT + (t + 1) * 128, j * 128:(j + 1) * 128], w=[qnb])
        oacc, oab = oacc_rot.next()
        if _dbg:
            P.v(lambda e, o=oacc: e.memset(o, 0.0), [], [oab])
        for g in range(2):
            gp = slice(g * 64, (g + 1) * 64)
            ncb = j // 4 + 1
            for cb in range(ncb):
                sb = T.s_rot.next()
                pl = 1 if cb == ncb - 1 else (0 if cb == ncb - 2 else None)
                first = True
                if pl is not None:
                    o0 = ((j % 4) * 2 + pl) * 384
                    P.mm(ps[sb][:, 0:384], C.ident_bf, tCMPB[:, o0:o0 + 384], True, False, [C.cb, tb], [psb[sb]])
                    first = False
                for t in range(3):
                    P.mm(ps[sb][:, t * 128:(t + 1) * 128], kcT[gp, cb * 128:(cb + 1) * 128], qn[gp, t, :],
                         first and t == 0, t == 2, [kcb, qnb], [psb[sb]])
                pt, ptb = pt_rot.next()
                P.act(pt, ps[sb][:, 0:384], AF.Exp, [psb[sb]], [ptb], scale=SC64)
                for t in range(3):
                    P.mm(ps[accs[t]][:, 0:193], pt[:, t * 128:(t + 1) * 128], VCO[:, cb, g, :], cb == 0, cb == ncb - 1,
                         [ptb, vcob], [psb[accs[t]]])
            for t in range(3):
                h = 3 * g + t
                a = accs[t]
                sm, smb = finalize_head(C, ps[a], psb[a], None, None, sm_rot, gate=gts[:, j, h * 3:h * 3 + 1], gb=tb,
                                        first=not _dbg, fdst=oacc[:, h * 64:(h + 1) * 64], fb=oab,
                                        skip=("c" not in _br))
                if t == 0:
                    P.v(lambda e, a=a, sm=sm: e.tensor_scalar(out=imp, in0=ps[a][:, 65:193], scalar1=sm[:, 0:1],
                                                              scalar2=None, op0=ALU.mult), [psb[a], smb], [swb])
                else:
                    P.v(lambda e, a=a, sm=sm: e.scalar_tensor_tensor(out=imp, in0=ps[a][:, 65:193], scalar=sm[:, 0:1],
                                                                     in1=imp, op0=ALU.mult, op1=ALU.add),
                        [psb[a], smb, swb], [swb])
            P.v(lambda e, j=j: e.tensor_tensor(out=score, in0=imp, in1=SELB[:, j, :], op=ALU.add), [swb, tb], [swb])
            P.v(lambda e: e.max(out=m8[:, 0:8], in_=score), [swb], [swb])
            P.v(lambda e: e.match_replace(out=sc2, in_to_replace=m8[:, 0:8], in_values=score, imm_value=-1e30),
                [swb], [swb])
            P.v(lambda e: e.max(out=m8[:, 8:16], in_=sc2), [swb], [swb])
            P.v(lambda e: e.tensor_scalar(out=selbf, in0=score, scalar1=m8[:, 15:16], scalar2=NEG,
                                          op0=ALU.is_lt, op1=ALU.mult), [swb], [swb])
            P.tr(ps[6][:, 0:128], selbf, C.ident_f, [swb, C.cb], [psb[6]])
            sel3, selb_ = sel_rot.next()
            for t in range(3):
                P.v(lambda e, t=t, sel3=sel3: e.tensor_copy(out=sel3[:, t * 128:(t + 1) * 128], in_=ps[6][:, 0:128]),
                    [psb[6]], [selb_])
            nkb = 4 * j + 4
            for kb in range(nkb):
                sb = T.s_rot.next()
                P.mm(ps[sb][:, 0:384], ETAB[:, kb * 128:(kb + 1) * 128], sel3, True, False, [etb, selb_], [psb[sb]])
                if kb >= 4 * j:
                    o0 = (kb - 4 * j) * 384
                    P.mm(ps[sb][:, 0:384], C.ident_bf, tDB3[:, o0:o0 + 384], False, False, [C.cb, tb], [psb[sb]])
                for t in range(3):
                    P.mm(ps[sb][:, t * 128:(t + 1) * 128], ksT[gp, kb % 4, (kb // 4) * 128:(kb // 4 + 1) * 128],
                         qn[gp, 3 + t, :], False, t == 2, [ksb, qnb], [psb[sb]])
                pt, ptb = pt_rot.next()
                P.act(pt, ps[sb][:, 0:384], AF.Exp, [psb[sb]], [ptb], scale=SC64)
                for t in range(3):
                    P.mm(ps[accs[t]][:, 0:65], pt[:, t * 128:(t + 1) * 128], Vn[:, kb % 4, kb // 4, g * 65:(g + 1) * 65],
                         kb == 0, kb == nkb - 1, [ptb, vnb], [psb[accs[t]]])
            for t in range(3):
                h = 3 * g + t
                a = accs[t]
                finalize_head(C, ps[a], psb[a], None, None, sm_rot, gate=gts[:, j, h * 3 + 1:h * 3 + 2], gb=tb,
                              first=False, fdst=oacc[:, h * 64:(h + 1) * 64], fb=oab, skip=("s" not in _br))
            kbs = [kb for kb in range(4 * j - 4, 4 * j + 4) if kb >= 0]
            for ki, kb in enumerate(kbs):
                sb = T.s_rot.next()
                o0 = (kb - (4 * j - 4)) * 384
                P.mm(ps[sb][:, 0:384], C.ident_bf, tWINB[:, o0:o0 + 384], True, False, [C.cb, tb], [psb[sb]])
                for t in range(3):
                    P.mm(ps[sb][:, t * 128:(t + 1) * 128], kwT[gp, kb % 4, (kb // 4) * 128:(kb // 4 + 1) * 128],
                         qn[gp, 3 + t, :], False, t == 2, [kwb, qnb], [psb[sb]])
                pt, ptb = pt_rot.next()
                P.act(pt, ps[sb][:, 0:384], AF.Exp, [psb[sb]], [ptb], scale=SC64)
                for t in range(3):
                    P.mm(ps[accs[t]][:, 0:65], pt[:, t * 128:(t + 1) * 128],
                         Vn[:, kb % 4, kb // 4, 130 + g * 65:130 + (g + 1) * 65],
                         ki == 0, ki == len(kbs) - 1, [ptb, vnb], [psb[accs[t]]])
            for t in range(3):
                h = 3 * g + t
                a = accs[t]
                finalize_head(C, ps[a], psb[a], None, None, sm_rot, gate=gts[:, j, h * 3 + 2:h * 3 + 3], gb=tb,
                              first=False, fdst=oacc[:, h * 64:(h + 1) * 64], fb=oab, skip=("w" not in _br))
        P.act(O[:, j, :], oacc, AF.Copy, [oab], [ob])
    if "dbg_nsa" in d:
        P.dma("sync", d["dbg_nsa"][:, :], O.rearrange("p j c -> p (j c)"), r=[ob])
    P.barrier()
    out_project(C, O, ob, 384, d["w_out"], 640, False)
    A.reset(m)


def phase_ln(C, gain_d, bias_d, want_xT=True):
    P, A = C.P, C.A
    m = A.mark()
    gain = A.f32(D_MODEL)
    bias = A.f32(D_MODEL)
    gb = Buf("gainbias")
    P.dma("sync", gain, gain_d.partition_broadcast(128), w=[gb])
    P.dma("sync", bias, bias_d.partition_broadcast(128), w=[gb])
    st_rot = Rot([(A.f32(16), Buf()) for _ in range(2)])
    xbf_rot = Rot([(A.bf(D_MODEL), Buf()) for _ in range(2)])
    for j in range(NT):
        layer_norm_tile(C, j, gain, bias, gb, st_rot)
        if want_xT:
            make_xT(C, j, xbf_rot)
    P.barrier()
    A.reset(m)


def phase_load_x_only(C, x_d):
    for j in range(NT):
        C.P.dma("sync", C.x_tok[:, j, :], x_d[j * 128:(j + 1) * 128, :], w=[C.xb[j]])


def phase_store_x(C, y_d):
    for j in range(NT):
        C.P.dma("sync", y_d[j * 128:(j + 1) * 128, :], C.x_tok[:, j, :], r=[C.xb[j]])
    C.P.barrier()


def build_program(spec):
    B = Builder(spec, False)
    with contextlib.ExitStack() as es:
        C = setup_common(B, es)
        nc = B.nc
        exch = {}
        for ph in spec:
            kind = ph[0]
            if kind == "load_x":
                phase_load_x(C, B.inp("x_in", [TOK, D_MODEL]))
            elif kind == "load_x_only":
                phase_load_x_only(C, B.inp("x_in", [TOK, D_MODEL]))
            elif kind == "ffn":
                _, l, f, want = ph
                sfx = "_%d%d" % (l, f)
                phase_ffn(C, B.inp("ffn_w_in" + sfx, [D_MODEL, 2 * D_FF]), B.inp("ffn_w_out" + sfx, [D_FF, D_MODEL]),
                          B.inp("ln_g" + sfx, [D_MODEL]), B.inp("ln_b" + sfx, [D_MODEL]), want_xT=want)
            elif kind == "proj":
                _, l, mode = ph
                sfx = "_%d" % l
                d = decl_proj_inputs(B, sfx)
                decl_rope_inputs(B, d)
                if mode == "out":
                    d["Q"] = B.outp("Qo", [Q_ROWS, TOK], BF16)
                    d["XK"] = B.outp("XKo", [XK_ROWS, TOK], BF16)
                    d["XV"] = B.outp("XVo", [TOK, XV_COLS], BF16)
                    d["gates"] = B.outp("gateso", [TOK, 18], F32)
                phase_proj(C, d)
            elif kind == "attn":
                _, l, mode = ph
                sfx = "_%d" % l
                d = decl_attn_inputs(B, sfx)
                if mode == "in":
                    d["Q"] = B.inp("Q", [Q_ROWS, TOK], BF16)
                    d["XKall"] = B.inp("XKall", [NR, XK_ROWS, TOK], BF16)
                    d["XVall"] = B.inp("XVall", [NR, TOK, XV_COLS], BF16)
                    d["gates"] = B.inp("gates", [TOK, 18], F32)
                phase_mla(C, d)
                phase_dil(C, d)
                phase_nsa(C, d)
            elif kind == "ln":
                _, l, i, want = ph
                sfx = "_%d%d" % (l, i)
                phase_ln(C, B.inp("lnm_g" + sfx, [D_MODEL]), B.inp("lnm_b" + sfx, [D_MODEL]), want_xT=want)
            elif kind == "store_x":
                phase_store_x(C, B.outp("x_out", [TOK, D_MODEL]))
            else:
                raise ValueError(kind)
        C.P.barrier()
        with nc.Block() as block:
            C.P.emit(block)
    return B


def host_ffn_inputs(inputs, l, f):
    sfx = "_%d%d" % (l, f)
    c = np.ascontiguousarray
    li = 0 if f == 0 else 2
    return {"ffn_w_in" + sfx: c(inputs["ffn_w_in"][l, f]), "ffn_w_out" + sfx: c(inputs["ffn_w_out"][l, f]),
            "ln_g" + sfx: c(inputs["ln_gain"][l, li]), "ln_b" + sfx: c(inputs["ln_bias"][l, li])}


def host_ln_inputs(inputs, l):
    sfx = "_%d%d" % (l, 1)
    c = np.ascontiguousarray
    return {"lnm_g" + sfx: c(inputs["ln_gain"][l, 1]), "lnm_b" + sfx: c(inputs["ln_bias"][l, 1])}


_PROGS = {}


def _get_prog(key, spec):
    if key not in _PROGS:
        _PROGS[key] = build_program(spec)
    return _PROGS[key]


def _run(B, maps):
    names = set(B.din.keys())
    in_maps = [{k: v for k, v in m.items() if k in names} for m in maps]
    for m in in_maps:
        missing = names - set(m.keys())
        assert not missing, missing
    res = run_bass_kernel_spmd(B.nc, in_maps, core_ids=list(range(8)))
    return res.results


def kernel(**inputs):
    inputs = {k: np.asarray(v) for k, v in inputs.items()}
    x = inputs["x"].astype(np.float32)
    consts = host_consts()
    aconsts = host_attn_consts()
    per_rank = []
    for r in range(NR):
        t = dict(host_rope_tables(r))
        t.update(host_attn_tables(r))
        per_rank.append(t)

    def base(c):
        m = dict(consts)
        m.update(aconsts)
        m.update(per_rank[c % 4])
        return m

    specA = [("load_x",), ("ffn", 0, 0, True), ("proj", 0, "out"), ("store_x",)]
    BA = _get_prog("A", specA)
    maps = []
    for c in range(8):
        m = base(c)
        m["x_in"] = own_rows(x, c)
        m.update(host_ffn_inputs(inputs, 0, 0))
        m.update(host_proj_weights(inputs, 0, "_0"))
        maps.append(m)
    resA = _run(BA, maps)

    def exchange(res):
        out = []
        for c in range(8):
            b = c // 4
            out.append({
                "x_in": np.asarray(res[c]["x_out"]),
                "Q": np.asarray(res[c]["Qo"]), "gates": np.asarray(res[c]["gateso"]),
                "XKall": np.stack([np.asarray(res[b * 4 + r]["XKo"]) for r in range(NR)]),
                "XVall": np.stack([np.asarray(res[b * 4 + r]["XVo"]) for r in range(NR)]),
            })
        return out

    specB = [("load_x_only",), ("attn", 0, "in"), ("ln", 0, 1, True), ("ffn", 0, 1, True), ("ffn", 1, 0, True),
             ("proj", 1, "out"), ("store_x",)]
    BB = _get_prog("B", specB)
    ex = exchange(resA)
    maps = []
    for c in range(8):
        m = base(c)
        m.update(ex[c])
        m.update(host_attn_weights(inputs, 0, "_0"))
        m.update(host_ln_inputs(inputs, 0))
        m.update(host_ffn_inputs(inputs, 0, 1))
        m.update(host_ffn_inputs(inputs, 1, 0))
        m.update(host_proj_weights(inputs, 1, "_1"))
        maps.append(m)
    resB = _run(BB, maps)

    specC = [("load_x_only",), ("attn", 1, "in"), ("ln", 1, 1, True), ("ffn", 1, 1, False), ("store_x",)]
    BC = _get_prog("C", specC)
    ex = exchange(resB)
    maps = []
    for c in range(8):
        m = base(c)
        m.update(ex[c])
        m.update(host_attn_weights(inputs, 1, "_1"))
        m.update(host_ln_inputs(inputs, 1))
        m.update(host_ffn_inputs(inputs, 1, 1))
        maps.append(m)
    resC = _run(BC, maps)

    out = np.zeros((2, SEQ, D_MODEL), np.float32)
    for c in range(8):
        b, r = c // 4, c % 4
        o = np.asarray(resC[c]["x_out"]).reshape(NT, 128, D_MODEL)
        out[b].reshape(NT, NR, 128, D_MODEL)[:, r] = o
    return out
```
